# Optimizing a Trainium2 kernel written in Bass

```python
import jax, jax.numpy as jnp
from jax import lax
import numpy as np

D_MODEL = 1024
BATCH = 4
SEQ = 4096
DEPTH = 2

N_META = 16
N_MIXERS = 2
SB_HEADS = 16
SB_HEAD_DIM = D_MODEL // SB_HEADS
Q_BLOCK = 128
LRU_WIDTH = D_MODEL
LRU_BLOCKS = 8
LRU_BLOCK_DIM = LRU_WIDTH // LRU_BLOCKS
CONV_WIDTH = 4
LRU_C = 8.0
D_FF = 4 * D_MODEL
EPS = 1e-6
N_SB_LAYERS = (DEPTH + 1) // 2
N_LRU_LAYERS = DEPTH // 2

kernel_name = "hybrid_stickbreak_rglru_meta"


def rms_norm(x, g):
    xf = x.astype(jnp.float32)
    y = xf * lax.rsqrt(jnp.mean(xf * xf, axis=-1, keepdims=True) + EPS)
    return (y * g.astype(jnp.float32)).astype(x.dtype)


def _sb_block(q_blk, k, v, q_pos):
    t_len = k.shape[2]
    k_pos = jnp.arange(t_len)
    z = jnp.einsum('bhqd,bhkd->bhqk', q_blk, k).astype(jnp.float32) * (SB_HEAD_DIM ** -0.5)
    causal = k_pos[None, :] < q_pos[:, None]
    log_keep = jnp.where(causal, -jax.nn.softplus(z), 0.0)
    after = lax.cumsum(log_keep, axis=3, reverse=True) - log_keep
    w = jnp.where(causal, jnp.exp(jax.nn.log_sigmoid(z) + after), 0.0)
    return jnp.einsum('bhqk,bhkd->bhqd', w.astype(v.dtype), v)


def stick_breaking_mixer(x, w_qkv, w_o):
    b, t_len, _ = x.shape
    qkv = (x @ w_qkv).reshape(b, t_len, 3, SB_HEADS, SB_HEAD_DIM).transpose(2, 0, 3, 1, 4)
    q, k, v = qkv[0], qkv[1], qkv[2]
    meta_out = _sb_block(q[:, :, :N_META], k, v, jnp.arange(N_META))
    n_blk = (t_len - N_META) // Q_BLOCK
    q_real = q[:, :, N_META:].reshape(b, SB_HEADS, n_blk, Q_BLOCK, SB_HEAD_DIM).transpose(2, 0, 1, 3, 4)
    pos = N_META + jnp.arange(n_blk * Q_BLOCK).reshape(n_blk, Q_BLOCK)
    real_out = lax.map(lambda a: _sb_block(a[0], k, v, a[1]), (q_real, pos))
    real_out = real_out.transpose(1, 2, 0, 3, 4).reshape(b, SB_HEADS, n_blk * Q_BLOCK, SB_HEAD_DIM)
    o = jnp.concatenate([meta_out, real_out], axis=2)
    o = o.transpose(0, 2, 1, 3).reshape(b, t_len, D_MODEL)
    return o @ w_o


def _lru_combine(left, right):
    a1, b1 = left
    a2, b2 = right
    return a1 * a2, a2 * b1 + b2


def rglru_mixer(x, w_in, conv_w, conv_b, w_rg, b_rg, w_ig, b_ig, lam, w_out):
    b, t_len, _ = x.shape
    gate_in, rec_in = jnp.split(x @ w_in, 2, axis=-1)
    gate = jax.nn.gelu(gate_in)
    xp = jnp.pad(rec_in, ((0, 0), (CONV_WIDTH - 1, 0), (0, 0)))
    u = conv_b + sum(xp[:, j:j + t_len] * conv_w[j] for j in range(CONV_WIDTH))
    ub = u.reshape(b, t_len, LRU_BLOCKS, LRU_BLOCK_DIM)
    r = jax.nn.sigmoid(jnp.einsum('btni,nij->btnj', ub, w_rg).reshape(b, t_len, LRU_WIDTH) + b_rg)
    i = jax.nn.sigmoid(jnp.einsum('btni,nij->btnj', ub, w_ig).reshape(b, t_len, LRU_WIDTH) + b_ig)
    log_a = (-LRU_C * jax.nn.softplus(-lam.astype(jnp.float32))) * r.astype(jnp.float32)
    a = jnp.exp(log_a)
    mult = jnp.sqrt(-jnp.expm1(2.0 * log_a))
    bt = mult * (i * u).astype(jnp.float32)
    _, h = lax.associative_scan(_lru_combine, (a, bt), axis=1)
    y = h.astype(x.dtype) * gate
    return y @ w_out


def sq_relu_mlp(x, w_up, w_down):
    hdn = jax.nn.relu(x @ w_up)
    return (hdn * hdn) @ w_down


def setup_inputs(seed: int = 0) -> dict:
    key = jax.random.key(seed)
    ks = jax.random.split(key, 20)
    f32 = jnp.float32
    D = D_MODEL

    def nrm(k, shape, scale):
        return jax.random.normal(k, shape, f32) * scale

    a0 = jax.random.uniform(ks[11], (N_LRU_LAYERS, LRU_WIDTH), f32, 0.9, 0.999)
    return {
        "x": nrm(ks[0], (BATCH, SEQ, D), 1.0),
        "meta_tokens": nrm(ks[1], (N_META, D), 1.0),
        "norm_mix": 1.0 + nrm(ks[2], (DEPTH, D), 0.02),
        "norm_mlp": 1.0 + nrm(ks[3], (DEPTH, D), 0.02),
        "sb_w_qkv": nrm(ks[4], (N_SB_LAYERS, D, 3 * D), D ** -0.5),
        "sb_w_o": nrm(ks[5], (N_SB_LAYERS, D, D), D ** -0.5),
        "lru_w_in": nrm(ks[6], (N_LRU_LAYERS, D, 2 * LRU_WIDTH), D ** -0.5),
        "lru_conv_w": nrm(ks[7], (N_LRU_LAYERS, CONV_WIDTH, LRU_WIDTH), CONV_WIDTH ** -0.5),
        "lru_conv_b": nrm(ks[8], (N_LRU_LAYERS, LRU_WIDTH), 0.01),
        "lru_w_rg": nrm(ks[9], (N_LRU_LAYERS, LRU_BLOCKS, LRU_BLOCK_DIM, LRU_BLOCK_DIM), LRU_BLOCK_DIM ** -0.5),
        "lru_b_rg": nrm(ks[10], (N_LRU_LAYERS, LRU_WIDTH), 0.01),
        "lru_w_ig": nrm(ks[12], (N_LRU_LAYERS, LRU_BLOCKS, LRU_BLOCK_DIM, LRU_BLOCK_DIM), LRU_BLOCK_DIM ** -0.5),
        "lru_b_ig": nrm(ks[13], (N_LRU_LAYERS, LRU_WIDTH), 0.01),
        "lru_lambda": jnp.log(a0) - jnp.log1p(-a0),
        "lru_w_out": nrm(ks[14], (N_LRU_LAYERS, LRU_WIDTH, D), LRU_WIDTH ** -0.5),
        "mlp_w_up": nrm(ks[15], (DEPTH, D, D_FF), D ** -0.5),
        "mlp_w_down": nrm(ks[16], (DEPTH, D_FF, D), D_FF ** -0.5),
        "norm_final": 1.0 + nrm(ks[17], (D,), 0.02),
    }


def reference(x, meta_tokens, norm_mix, norm_mlp, sb_w_qkv, sb_w_o, lru_w_in, lru_conv_w,
              lru_conv_b, lru_w_rg, lru_b_rg, lru_w_ig, lru_b_ig, lru_lambda, lru_w_out,
              mlp_w_up, mlp_w_down, norm_final):
    b = x.shape[0]
    meta = jnp.broadcast_to(meta_tokens[None].astype(x.dtype), (b, N_META, D_MODEL))
    h = jnp.concatenate([meta, x], axis=1)
    for i in range(DEPTH):
        hn = rms_norm(h, norm_mix[i])
        j = i // N_MIXERS
        if i % N_MIXERS == 0:
            h = h + stick_breaking_mixer(hn, sb_w_qkv[j], sb_w_o[j])
        else:
            h = h + rglru_mixer(hn, lru_w_in[j], lru_conv_w[j], lru_conv_b[j], lru_w_rg[j],
                                lru_b_rg[j], lru_w_ig[j], lru_b_ig[j], lru_lambda[j], lru_w_out[j])
        hn = rms_norm(h, norm_mlp[i])
        h = h + sq_relu_mlp(hn, mlp_w_up[i], mlp_w_down[i])
    h = rms_norm(h, norm_final)
    return h[:, N_META:]
```

```python
import numpy as np
from contextlib import ExitStack
import concourse.bass as bass
import concourse.mybir as mybir
from concourse.bass_utils import run_bass_kernel_spmd
import ml_dtypes

F32 = mybir.dt.float32
BF16 = mybir.dt.bfloat16
I32 = mybir.dt.int32
AF = mybir.ActivationFunctionType
ALU = mybir.AluOpType

NCH = 9
T = 16 + 512 * (NCH - 1)
NKT = 1 + 4 * (NCH - 1)


def set_nch(n):
    global NCH, T, NKT
    NCH = n
    T = 16 + 512 * (n - 1)
    NKT = 1 + 4 * (n - 1)
NMETA = 16
D = 1024
TH = 2064
EPS = 1e-6
ENGS = ("tensor", "vector", "scalar", "gpsimd", "sync")
SAME_ENGINE_SYNC = True


class SemC:
    def __init__(self, h, name):
        self.h = h
        self.n = 0
        self.name = name


class Buf:
    def __init__(self, name, track_reads=True):
        self.name = name
        self.wr = None
        self.rd = []
        self.track_reads = track_reads


class Prog:
    def __init__(self, nc, es):
        self.nc = nc
        self.es = es
        self.q = {e: [] for e in ENGS}
        self.esem = {e: self.newsem("p_" + e) for e in ENGS}
        self.nsem = len(ENGS)

    def newsem(self, name):
        h = self.es.enter_context(self.nc.semaphore(name))
        return SemC(h, name)

    def group(self, eng, fns, reads=(), writes=(), dma_sem=None, extra_waits=(), force_sync=False, sig=True):
        waits = []
        for b in reads:
            if b.wr is not None:
                waits.append(b.wr)
        for b in writes:
            if b.wr is not None:
                waits.append(b.wr)
            waits.extend(b.rd)
        waits.extend(w for w in extra_waits if w is not None)
        tok = None
        if dma_sem is not None:
            assert len(fns) == 1
            dma_sem.n += 16
            tok = (dma_sem, dma_sem.n, None)
            sigspec = (dma_sem, 16)
        elif sig:
            s = self.esem[eng]
            s.n += 1
            tok = (s, s.n, eng)
            sigspec = (s, 1)
        else:
            sigspec = None
        n = len(fns)
        for i, fn in enumerate(fns):
            self.q[eng].append((fn, waits if i == 0 else [], sigspec if i == n - 1 else None, force_sync))
        if tok is not None:
            for b in writes:
                b.wr = tok
                b.rd = []
            for b in reads:
                if b.track_reads:
                    b.rd.append(tok)
        return tok

    def op(self, eng, fn, reads=(), writes=(), dma_sem=None, extra_waits=(), force_sync=False, sig=True):
        return self.group(eng, [fn], reads=reads, writes=writes, dma_sem=dma_sem, extra_waits=extra_waits,
                          force_sync=force_sync, sig=sig)

    def barrier(self, extra=()):
        toks = [(self.esem[x], self.esem[x].n, x) for x in ENGS if self.esem[x].n > 0]
        toks.extend(t for t in extra if t is not None)
        for e in ENGS:
            self.q[e].append((None, [t for t in toks if t[2] != e], None, False))

    def replay(self, block):
        nc = self.nc
        prog = self

        def run(engname, e):
            waited = {}
            for fn, waits, sigspec, force_sync in prog.q[engname]:
                for (s, v, src) in waits:
                    if src == engname and not (SAME_ENGINE_SYNC or force_sync):
                        continue
                    if waited.get(s.name, 0) >= v:
                        continue
                    e.wait_ge(s.h, v)
                    waited[s.name] = v
                if fn is None:
                    continue
                ins = fn(e)
                if sigspec is not None:
                    ins.then_inc(sigspec[0].h, sigspec[1])

        @block.tensor
        def _(e):
            run("tensor", e)

        @block.vector
        def _(e):
            run("vector", e)

        @block.scalar
        def _(e):
            run("scalar", e)

        @block.gpsimd
        def _(e):
            run("gpsimd", e)

        @block.sync
        def _(e):
            run("sync", e)


def chunk_cols(c):
    return (0, 16) if c == 0 else (16 + 512 * (c - 1), 16 + 512 * c)


def tile_rows(i):
    return (0, 16) if i == 0 else (16 + 128 * (i - 1), 16 + 128 * i)


def make_consts():
    j = np.arange(128)[:, None]
    s = np.arange(128)[None, :]
    ident = (j == s).astype(np.float32)
    neg = np.where(s <= j, -30000.0, 0.0).astype(np.float32)
    u = (j < s).astype(np.float32)
    ones = np.ones((128, 128), np.float32)
    return np.concatenate([ident, neg, u, ones], axis=1).astype(ml_dtypes.bfloat16)


class Ctx:
    pass


def setup_common(nc, es, P):
    cx = Ctx()
    cx.nc, cx.es, cx.P = nc, es, P
    cx.consts_d = nc.dram_tensor("consts", [128, 512], BF16, kind="ExternalInput").ap()
    cx.consts = es.enter_context(nc.sbuf_tensor("consts_sb", [128, 512], BF16))
    cx.cvec = es.enter_context(nc.sbuf_tensor("cvec", [128, 4], F32))
    cx.ps = [es.enter_context(nc.psum_tensor(f"ps{i}", [128, 512], F32)) for i in range(8)]
    cx.psb = [Buf(f"ps{i}") for i in range(8)]
    cx.b_consts = Buf("consts", track_reads=False)
    cx.b_cvec = Buf("cvec", track_reads=False)
    cx.s_const = P.newsem("d_const")
    P.op("sync", lambda e: e.dma_start(out=cx.consts[:], in_=cx.consts_d[:, :]), writes=[cx.b_consts], dma_sem=cx.s_const)
    P.group("vector", [lambda e: e.memset(cx.cvec[:, 0:1], 1.0), lambda e: e.memset(cx.cvec[:, 1:2], EPS),
                       lambda e: e.memset(cx.cvec[:, 2:4], 0.0)], writes=[cx.b_cvec])
    cx.IDENT = cx.consts[:, 0:128]
    cx.NEG = cx.consts[:, 128:256]
    cx.U = cx.consts[:, 256:384]
    cx.ONES = cx.consts[:, 384:512]
    return cx


def emit_norm(cx, x_of_k, xbuf, N, gcol_of_k, gbuf, hn_of_k, hnbuf, sq, sqbuf, tmpa, tmpabuf, rstd, rstdbuf, psbank):
    P = cx.P
    ps = cx.ps[psbank]
    pb = cx.psb[psbank]
    P.group("gpsimd", [(lambda e, k=k: e.tensor_tensor(out=sq(k), in0=x_of_k(k), in1=x_of_k(k), op=ALU.mult)) for k in range(8)],
            reads=[xbuf], writes=[sqbuf])
    P.group("tensor", [(lambda e, k=k: e.matmul(ps[:, 0:N], cx.ONES, sq(k), start=(k == 0), stop=(k == 7))) for k in range(8)],
            reads=[sqbuf, cx.b_consts], writes=[pb])
    P.op("scalar", lambda e: e.activation(out=tmpa, in_=ps[:, 0:N], func=AF.Ln, bias=cx.cvec[:, 1:2], scale=1.0 / D),
         reads=[pb, cx.b_cvec], writes=[tmpabuf])
    P.op("scalar", lambda e: e.activation(out=rstd, in_=tmpa, func=AF.Exp, scale=-0.5),
         reads=[tmpabuf], writes=[rstdbuf], force_sync=True)
    P.group("vector", [(lambda e, k=k: e.scalar_tensor_tensor(out=hn_of_k(k), in0=x_of_k(k), scalar=gcol_of_k(k), in1=rstd,
                                                              op0=ALU.mult, op1=ALU.mult)) for k in range(8)],
            reads=[xbuf, rstdbuf, gbuf], writes=[hnbuf])


def emit_phase1(cx, out_writer):
    nc, es, P = cx.nc, cx.es, cx.P
    xT_d = nc.dram_tensor("xT", [D, T], F32, kind="ExternalInput").ap()
    wqkv_d = nc.dram_tensor("wqkv", [D, 1536], F32, kind="ExternalInput").ap()
    g0_d = nc.dram_tensor("g_mix0", [128, 8], F32, kind="ExternalInput").ap()

    st1 = ExitStack()
    es.enter_context(st1)
    A = lambda name, shape, dt: st1.enter_context(nc.sbuf_tensor(name, shape, dt))
    W = A("wqkv_sb", [128, 8, 1536], BF16)
    QT = A("QT", [128, 4, T], BF16)
    KT = A("KT", [128, 4, T], BF16)
    V = A("V", [128, NKT, 512], BF16)
    g0 = A("g0", [128, 8], F32)
    b_W = Buf("W", track_reads=False)
    b_g0 = Buf("g0", track_reads=False)
    b_QT = [Buf(f"QT{c}", track_reads=False) for c in range(NCH)]
    b_KT = [Buf(f"KT{c}", track_reads=False) for c in range(NCH)]
    b_V = [Buf(f"V{i}", track_reads=False) for i in range(NKT)]
    s_w = P.newsem("d_w")
    xT_v = xT_d.rearrange("(k p) t -> p k t", p=128)
    wq_v = wqkv_d.rearrange("(k p) n -> p k n", p=128)
    toks = []
    for k in range(8):
        toks.append(P.op("gpsimd", (lambda e, k=k: e.dma_start(out=W[:, k, :], in_=wq_v[:, k, :])), dma_sem=s_w))
    b_W.wr = toks[-1]
    P.op("sync", lambda e: e.dma_start(out=g0[:], in_=g0_d[:, :]), writes=[b_g0], dma_sem=P.newsem("d_g0"))

    st1a = ExitStack()
    xs = [st1a.enter_context(nc.sbuf_tensor(f"xs{i}", [128, 8, 512], F32)) for i in range(2)]
    b_xs = [Buf(f"xs{i}") for i in range(2)]
    s_xs = [P.newsem(f"d_xs{i}") for i in range(2)]
    sq = st1a.enter_context(nc.sbuf_tensor("sq", [128, 8, 512], BF16))
    b_sq = Buf("sq")
    hn = [st1a.enter_context(nc.sbuf_tensor(f"hn{i}", [128, 8, 512], BF16)) for i in range(2)]
    b_hn = [Buf(f"hn{i}") for i in range(2)]
    tmpa = st1a.enter_context(nc.sbuf_tensor("tmpa", [128, 512], F32))
    rstd = st1a.enter_context(nc.sbuf_tensor("rstd", [128, 512], F32))
    b_tmpa, b_rstd = Buf("tmpa"), Buf("rstd")
    evq = 0
    for c in range(NCH):
        c0, c1 = chunk_cols(c)
        N = c1 - c0
        sl = c % 2
        P.op("sync", (lambda e, sl=sl, c0=c0, c1=c1, N=N: e.dma_start(out=xs[sl][:, :, 0:N], in_=xT_v[:, :, c0:c1])),
             writes=[b_xs[sl]], dma_sem=s_xs[sl])
        emit_norm(cx, (lambda k, sl=sl, N=N: xs[sl][:, k, 0:N]), b_xs[sl], N,
                  (lambda k: g0[:, k:k + 1]), b_g0,
                  (lambda k, sl=sl, N=N: hn[sl][:, k, 0:N]), b_hn[sl],
                  (lambda k, N=N: sq[:, k, 0:N]), b_sq, tmpa[:, 0:N], b_tmpa, rstd[:, 0:N], b_rstd, psbank=0)
        for hp in range(4):
            for which in range(2):
                bank = 1 + (evq % 3)
                evq += 1
                col0 = which * 512 + hp * 128
                ps = cx.ps[bank]
                P.group("tensor", [(lambda e, k=k, ps=ps, col0=col0, sl=sl, N=N: e.matmul(ps[:, 0:N], W[:, k, col0:col0 + 128], hn[sl][:, k, 0:N],
                                                                                       start=(k == 0), stop=(k == 7))) for k in range(8)],
                        reads=[b_hn[sl], b_W], writes=[cx.psb[bank]])
                if which == 0:
                    P.op("scalar", (lambda e, ps=ps, hp=hp, c0=c0, c1=c1, N=N: e.activation(out=QT[:, hp, c0:c1], in_=ps[:, 0:N], func=AF.Copy, scale=0.125)),
                         reads=[cx.psb[bank]], writes=[b_QT[c]])
                else:
                    P.op("vector", (lambda e, ps=ps, hp=hp, c0=c0, c1=c1, N=N: e.tensor_copy(out=KT[:, hp, c0:c1], in_=ps[:, 0:N])),
                         reads=[cx.psb[bank]], writes=[b_KT[c]])
        tiles = [0] if c == 0 else [4 * (c - 1) + 1 + j for j in range(4)]
        for ti, tl in enumerate(tiles):
            r0, r1 = tile_rows(tl)
            rows = r1 - r0
            l0 = r0 - c0
            bank = 4 + (ti % 2)
            ps = cx.ps[bank]
            P.group("tensor", [(lambda e, k=k, ps=ps, sl=sl, l0=l0, rows=rows: e.matmul(ps[0:rows, :], hn[sl][:, k, l0:l0 + rows], W[:, k, 1024:1536],
                                                                                   start=(k == 0), stop=(k == 7))) for k in range(8)],
                    reads=[b_hn[sl], b_W], writes=[cx.psb[bank]])
            eng = "vector" if ti % 2 == 0 else "scalar"
            if eng == "vector":
                P.op("vector", (lambda e, ps=ps, tl=tl, rows=rows: e.tensor_copy(out=V[0:rows, tl, :], in_=ps[0:rows, :])),
                     reads=[cx.psb[bank]], writes=[b_V[tl]])
            else:
                P.op("scalar", (lambda e, ps=ps, tl=tl, rows=rows: e.activation(out=V[0:rows, tl, :], in_=ps[0:rows, :], func=AF.Copy)),
                     reads=[cx.psb[bank]], writes=[b_V[tl]])
    P.barrier()
    st1a.close()

    st1b = ExitStack()
    B_ = lambda name, shape, dt: st1b.enter_context(nc.sbuf_tensor(name, shape, dt))
    E = [B_(f"E{h}", [128, 512], F32) for h in range(2)]
    SP = [[B_(f"SP{s}{h}", [128, 512], BF16) for h in range(2)] for s in range(2)]
    ARG = [[B_(f"ARG{s}{h}", [128, 512], F32) for h in range(2)] for s in range(2)]
    WW = [[B_(f"WW{s}{h}", [128, 512], BF16) for h in range(2)] for s in range(2)]
    OST = [B_(f"OST{s}", [128, 4, 512], BF16) for s in range(2)]
    CAR = [B_(f"CAR{h}", [128, 512], F32) for h in range(2)]
    b_CAR = [Buf(f"CAR{h}") for h in range(2)]
    b_E = [Buf(f"E{h}") for h in range(2)]
    b_SP = [[Buf(f"SP{s}{h}") for h in range(2)] for s in range(2)]
    b_ARG = [[Buf(f"ARG{s}{h}") for h in range(2)] for s in range(2)]
    b_WW = [[Buf(f"WW{s}{h}") for h in range(2)] for s in range(2)]
    b_OST = [Buf(f"OST{s}") for s in range(2)]
    Abank = lambda s, hh: 2 * s + hh
    Bbank = lambda hh: 4 + hh
    Obank = lambda o: 6 + o
    b_O = [[Buf(f"O{o}{hh}") for hh in range(2)] for o in range(2)]

    steps = []
    rnd = 0
    for c in range(NCH):
        c0, c1 = chunk_cols(c)
        for hp in range(4):
            lst = []
            if c == 0:
                lst.append(dict(kt=0, rows=16, n0=0, N=16, diag=True))
            else:
                for j in (3, 2, 1, 0):
                    lst.append(dict(kt=4 * (c - 1) + 1 + j, rows=128, n0=128 * j, N=512, diag=True))
                for kt in range(4 * (c - 1), 0, -1):
                    lst.append(dict(kt=kt, rows=128, n0=0, N=512, diag=False))
                lst.append(dict(kt=0, rows=16, n0=0, N=512, diag=False))
            for i, d in enumerate(lst):
                d.update(c=c, hp=hp, c0=c0, first=(i == 0), last=(i == len(lst) - 1), rnd=rnd)
                steps.append(d)
            rnd += 1

    def kchunk(kt):
        return 0 if kt == 0 else (kt - 1) // 4 + 1

    def S1(i, d):
        sl = i % 2
        c, hp, kt, rows, n0, N = d["c"], d["hp"], d["kt"], d["rows"], d["n0"], d["N"]
        r0, r1 = tile_rows(kt)
        for hh in range(2):
            pb = 64 * hh
            ps = cx.ps[Abank(sl, hh)]
            fns = [lambda e, ps=ps, pb=pb, hp=hp, r0=r0, r1=r1, rows=rows, n0=n0, N=N, c0=d["c0"], dg=d["diag"]:
                   e.matmul(ps[0:rows, n0:N], KT[pb:pb + 64, hp, r0:r1], QT[pb:pb + 64, hp, c0 + n0:c0 + N], start=True, stop=True)]
            if d["diag"]:
                fns.append(lambda e, ps=ps, rows=rows, n0=n0: e.matmul(ps[0:rows, n0:n0 + rows], cx.IDENT[0:rows, 0:rows], cx.NEG[0:rows, 0:rows],
                                                                        start=False, stop=True, skip_group_check=True))
            P.group("tensor", fns, reads=[b_KT[kchunk(kt)], b_QT[c], cx.b_consts], writes=[cx.psb[Abank(sl, hh)]])

    def S2(i, d):
        sl = i % 2
        rows, n0, N = d["rows"], d["n0"], d["N"]
        for hh in range(2):
            ps = cx.ps[Abank(sl, hh)]
            P.op("scalar", (lambda e, ps=ps, hh=hh, rows=rows, n0=n0, N=N: e.activation(out=E[hh][0:rows, n0:N], in_=ps[0:rows, n0:N], func=AF.Exp)),
                 reads=[cx.psb[Abank(sl, hh)]], writes=[b_E[hh]])
        for hh in range(2):
            P.op("scalar", (lambda e, hh=hh, sl=sl, rows=rows, n0=n0, N=N: e.activation(out=SP[sl][hh][0:rows, n0:N], in_=E[hh][0:rows, n0:N], func=AF.Ln,
                                                                                      bias=cx.cvec[0:rows, 0:1], scale=1.0)),
                 reads=[b_E[hh], cx.b_cvec], writes=[b_SP[sl][hh]])

    def split_ranges(d):
        if d["diag"]:
            n0, N, rows = d["n0"], d["N"], d["rows"]
            r = [(n0, n0 + rows, True)]
            if n0 + rows < N:
                r.append((n0 + rows, N, False))
            return r
        return [(0, d["N"], False)]

    def S3(i, d):
        sl = i % 2
        rows, n0, N = d["rows"], d["n0"], d["N"]
        for hh in range(2):
            psA = cx.ps[Abank(sl, hh)]
            psB = cx.ps[Bbank(hh)]
            fns = [lambda e, psA=psA, sl=sl, hh=hh, rows=rows, n0=n0, N=N: e.matmul(psA[0:rows, n0:N], cx.U[0:rows, 0:rows], SP[sl][hh][0:rows, n0:N],
                                                                                  start=False, stop=True, skip_group_check=True),
                   lambda e, psB=psB, sl=sl, hh=hh, rows=rows, n0=n0, N=N: e.matmul(psB[:, n0:N], cx.ONES[0:rows, :], SP[sl][hh][0:rows, n0:N],
                                                                                  start=True, stop=True)]
            P.group("tensor", fns, reads=[b_SP[sl][hh]], writes=[cx.psb[Abank(sl, hh)], cx.psb[Bbank(hh)]])

    def S4(i, d):
        sl = i % 2
        rows, n0, N = d["rows"], d["n0"], d["N"]
        if d["first"]:
            for hh in range(2):
                P.op("gpsimd", (lambda e, hh=hh: e.memset(CAR[hh][:, :], 0.0)), writes=[b_CAR[hh]])
        for hh in range(2):
            psB = cx.ps[Bbank(hh)]
            P.op("vector", (lambda e, psB=psB, hh=hh, rows=rows, n0=n0, N=N:
                            e.tensor_tensor(out=CAR[hh][0:rows, n0:N], in0=CAR[hh][0:rows, n0:N], in1=psB[0:rows, n0:N], op=ALU.add)),
                 reads=[cx.psb[Bbank(hh)]], writes=[b_CAR[hh]])
        for hh in range(2):
            psA = cx.ps[Abank(sl, hh)]
            P.op("vector", (lambda e, psA=psA, sl=sl, hh=hh, rows=rows, n0=n0, N=N:
                            e.tensor_tensor(out=ARG[sl][hh][0:rows, n0:N], in0=psA[0:rows, n0:N], in1=CAR[hh][0:rows, n0:N], op=ALU.subtract)),
                 reads=[cx.psb[Abank(sl, hh)], b_CAR[hh]], writes=[b_ARG[sl][hh]])

    def S5(i, d):
        sl = i % 2
        rows, n0, N = d["rows"], d["n0"], d["N"]
        for hh in range(2):
            P.op("scalar", (lambda e, sl=sl, hh=hh, rows=rows, n0=n0, N=N: e.activation(out=WW[sl][hh][0:rows, n0:N], in_=ARG[sl][hh][0:rows, n0:N], func=AF.Exp)),
                 reads=[b_ARG[sl][hh]], writes=[b_WW[sl][hh]])

    def S6(i, d):
        sl = i % 2
        rows, kt, hp = d["rows"], d["kt"], d["hp"]
        o = d["rnd"] % 2
        psO = cx.ps[Obank(o)]
        for hh in range(2):
            hcol = (hp * 2 + hh) * 64
            n0, N, first = d["n0"], d["N"], d["first"]
            fns = [lambda e, psO=psO, sl=sl, hh=hh, rows=rows, n0=n0, N=N, first=first, kt=kt, hcol=hcol:
                   e.matmul(psO[64 * hh:64 * hh + 64, n0:N], V[0:rows, kt, hcol:hcol + 64], WW[sl][hh][0:rows, n0:N], start=first, stop=True,
                            skip_group_check=True)]
            P.group("tensor", fns, reads=[b_WW[sl][hh], b_V[kt]], writes=[b_O[o][hh]])
        if d["last"]:
            c, N = d["c"], d["N"]
            ost = OST[c % 2]
            P.op("vector", (lambda e, psO=psO, ost=ost, hp=hp, N=N: e.tensor_copy(out=ost[:, hp, 0:N], in_=psO[:, 0:N])),
                 reads=[b_O[o][0], b_O[o][1]], writes=[b_OST[c % 2]])
            if hp == 3:
                out_writer(c, ost, b_OST[c % 2], N)

    n = len(steps)
    for s in range(-1, n + 1):
        if 0 <= s + 1 < n:
            S1(s + 1, steps[s + 1])
        if 0 <= s < n:
            S2(s, steps[s])
            S3(s, steps[s])
            S4(s, steps[s])
        if 0 <= s - 1 < n:
            S5(s - 1, steps[s - 1])
            S6(s - 1, steps[s - 1])
    cx.phase1_end = lambda extra=(): P.barrier(extra)
    st1b.close()
    st1.close()


def build_phase1_program():
    nc = bass.Bass("TRN2", target_bir_lowering=False)
    with ExitStack() as es:
        P = Prog(nc, es)
        cx = setup_common(nc, es, P)
        oT_d = nc.dram_tensor("oT", [512, T], BF16, kind="ExternalOutput").ap()
        oT_v = oT_d.rearrange("(h p) t -> p h t", p=128)
        s_out = [P.newsem("d_out0"), P.newsem("d_out1")]
        outtoks = {}

        def out_writer(c, ost, bost, N):
            c0, c1 = chunk_cols(c)
            outtoks[c % 2] = P.op("sync", (lambda e: e.dma_start(out=oT_v[:, :, c0:c1], in_=ost[:, :, 0:N])), reads=[bost], dma_sem=s_out[c % 2])

        emit_phase1(cx, out_writer)
        P.op("sync", lambda e: e.wait_ge(s_out[0].h, s_out[0].n), extra_waits=list(outtoks.values()), sig=False)
        block = es.enter_context(nc.Block())
        P.replay(block)
    return nc


NL = 4
TH = 16 + 512 * NL
NVEC = 13


def set_nl(n):
    global NL, TH
    NL = n
    TH = 16 + 512 * n


def lchunk(l):
    return (0, 16) if l == 0 else (16 + 512 * (l - 1), 16 + 512 * l)


V_GMIX0, V_GMLP0, V_GMIX1, V_GMLP1, V_GFIN, V_CW0, V_CW1, V_CW2, V_CW3, V_CB, V_BRG, V_BIG, V_LAM = range(13)


def emit_phase234(cx, load_o, n_ranks=8, mode="full"):
    nc, es, P = cx.nc, cx.es, cx.P
    dT = lambda name, shape, dt=F32: nc.dram_tensor(name, shape, dt, kind="ExternalInput").ap()
    useA = mode in ("full", "A")
    useB = mode in ("full", "B")
    r3 = lambda ap: ap.rearrange("(k p) n -> p k n", p=128)
    xTh_d = dT("xTh", [D, TH]).rearrange("(k p) t -> p k t", p=128) if useA else None
    w_o_d = r3(dT("w_o", [D, D])) if useA else None
    w_up_d = [r3(dT("w_up0", [D, 4 * D])) if useA else None, r3(dT("w_up1", [D, 4 * D])) if useB else None]
    w_dn_d = [dT("w_dn0", [4 * D, D]).rearrange("(h p) n -> p h n", p=128) if useA else None,
              dT("w_dn1", [4 * D, D]).rearrange("(h p) n -> p h n", p=128) if useB else None]
    w_in_d = r3(dT("w_in", [D, 2 * D])) if useA else None
    w_out_d = r3(dT("w_out", [D, D])) if useB else None
    w_rg_d = dT("w_rg", [8, 128, 128]).rearrange("n i j -> i n j") if useA else None
    w_ig_d = dT("w_ig", [8, 128, 128]).rearrange("n i j -> i n j") if useA else None
    vecs_d = dT("vecs", [128, NVEC * 8 + 8])
    idx_d = dT("idx_st", [128, 1], I32) if mode == "full" else None
    outT_d = nc.dram_tensor("outT", [D, TH], F32, kind="ExternalOutput").ap().rearrange("(k p) t -> p k t", p=128) if useB else None
    if mode == "full":
        hsp_h = nc.dram_tensor("hsp", [128, 8 * TH], F32)
    else:
        hsp_h = nc.dram_tensor("hsp", [128, 8 * TH], F32, kind=("ExternalOutput" if mode == "A" else "ExternalInput"))
        kio = "ExternalOutput" if mode == "A" else "ExternalInput"
        yl_h = nc.dram_tensor("yl", [128, 8 * TH], BF16, kind=kio)
        pg_h = nc.dram_tensor("pg", [128, 8 * TH], BF16, kind=kio)
        sto_h = nc.dram_tensor("st_own", [128, 16], F32, kind=kio)
        if mode == "B":
            sgi_h = nc.dram_tensor("sg", [128, 16], F32, kind="ExternalInput")
    if mode == "full":
        st_in_h = nc.dram_tensor("st_in", [128, 16], F32)
        st_all_h = nc.dram_tensor("st_all", [n_ranks * 128, 16], F32)

    st = ExitStack()
    es.enter_context(st)
    A = lambda name, shape, dt: st.enter_context(nc.sbuf_tensor(name, shape, dt))
    HN = A("HN", [128, 8, TH], BF16)
    OT = A("OT", [128, 8, TH], BF16)
    WA = A("WA", [128, 8, 1024], BF16)
    vecs = A("vecs_sb", [128, NVEC * 8 + 8], F32)
    sq = A("sq2", [128, 8, 512], BF16)
    tmpa = A("tmpa2", [128, 512], F32)
    rstd = A("rstd2", [128, 512], F32)
    b_HN = [Buf(f"HN{l}") for l in range(NL + 1)]
    b_OT = [Buf(f"OT{l}") for l in range(NL + 1)]
    b_WA = Buf("WA")
    s_WA = P.newsem("d_WA")
    b_vecs = Buf("vecs", track_reads=False)
    b_sq, b_tmpa, b_rstd = Buf("sq2"), Buf("tmpa2"), Buf("rstd2")
    vcol = lambda v, k: vecs[:, v * 8 + k:v * 8 + k + 1]
    FLAGB = vecs[:, NVEC * 8:NVEC * 8 + 1]
    P.op("sync", lambda e: e.dma_start(out=vecs[:], in_=vecs_d[:, :]), writes=[b_vecs], dma_sem=P.newsem("d_vecs"))
    if mode != "B":
        load_o(OT, b_OT)

    psrot = [0]

    def nextbank(lo=0, hi=8):
        b = lo + psrot[0] % (hi - lo)
        psrot[0] += 1
        return b

    hctx = {}

    def open_H(src_ap, tag):
        sth = ExitStack()
        Ht = sth.enter_context(nc.sbuf_tensor("H" + tag, [128, 8, TH], F32))
        b_H = [Buf(f"H{l}") for l in range(NL + 1)]
        tk = P.op("sync", lambda e: e.dma_start(out=Ht[:], in_=src_ap), dma_sem=P.newsem("d_H" + tag))
        for b in b_H:
            b.wr = tk
        hctx.update(st=sth, H=Ht, b_H=b_H)

    def load_WA(src_ap):
        P.op("gpsimd", (lambda e: e.dma_start(out=WA[:], in_=src_ap)), writes=[b_WA], dma_sem=s_WA)

    def proj_add(src, b_src):
        H, b_H = hctx["H"], hctx["b_H"]
        for l in range(NL + 1):
            c0, c1 = lchunk(l)
            N = c1 - c0
            for m in range(8):
                bank = nextbank()
                ps = cx.ps[bank]
                P.group("tensor", [(lambda e, k=k, ps=ps, m=m, c0=c0, c1=c1, N=N: e.matmul(ps[:, 0:N], WA[:, k, m * 128:(m + 1) * 128], src[:, k, c0:c1],
                                                                                        start=(k == 0), stop=(k == 7))) for k in range(8)],
                        reads=[b_WA, b_src[l]], writes=[cx.psb[bank]])
                P.op("vector", (lambda e, ps=ps, m=m, c0=c0, c1=c1, N=N: e.tensor_tensor(out=H[:, m, c0:c1], in0=H[:, m, c0:c1], in1=ps[:, 0:N], op=ALU.add)),
                     reads=[cx.psb[bank]], writes=[b_H[l]])

    def norm_all(gv):
        H, b_H = hctx["H"], hctx["b_H"]
        for l in range(NL + 1):
            c0, c1 = lchunk(l)
            N = c1 - c0
            emit_norm(cx, (lambda k, c0=c0, c1=c1: H[:, k, c0:c1]), b_H[l], N, (lambda k: vcol(gv, k)), b_vecs,
                      (lambda k, c0=c0, c1=c1: HN[:, k, c0:c1]), b_HN[l], (lambda k, N=N: sq[:, k, 0:N]), b_sq,
                      tmpa[:, 0:N], b_tmpa, rstd[:, 0:N], b_rstd, psbank=nextbank())

    def mlp(i, gv):
        H, b_H = hctx["H"], hctx["b_H"]
        norm_all(gv)
        stm = ExitStack()
        WU = [stm.enter_context(nc.sbuf_tensor(f"WU{i}_{s}", [128, 8, 512], BF16)) for s in range(2)]
        WD = [stm.enter_context(nc.sbuf_tensor(f"WD{i}_{s}", [128, 4, 1024], BF16)) for s in range(2)]
        RL = [stm.enter_context(nc.sbuf_tensor(f"RL{i}_{s}", [128, 512], BF16)) for s in range(2)]
        HD = [stm.enter_context(nc.sbuf_tensor(f"HD{i}_{s}", [128, 4, 512], BF16)) for s in range(2)]
        b_WU = [Buf(f"WU{s}") for s in range(2)]
        b_WD = [Buf(f"WD{s}") for s in range(2)]
        b_RL = [Buf(f"RL{s}") for s in range(2)]
        b_HD = [Buf(f"HD{s}") for s in range(2)]
        s_WU = [P.newsem(f"d_WU{i}_{s}") for s in range(2)]
        s_WD = [P.newsem(f"d_WD{i}_{s}") for s in range(2)]
        NG = 8

        def loadw(j):
            s = j % 2
            P.op("gpsimd", (lambda e, s=s, j=j: e.dma_start(out=WU[s][:], in_=w_up_d[i][:, :, j * 512:(j + 1) * 512])), writes=[b_WU[s]], dma_sem=s_WU[s])
            P.op("gpsimd", (lambda e, s=s, j=j: e.dma_start(out=WD[s][:], in_=w_dn_d[i][:, j * 4:(j + 1) * 4, :])), writes=[b_WD[s]], dma_sem=s_WD[s])

        loadw(0)
        rl_i = 0
        hd_i = 0
        for j in range(NG):
            if j + 1 < NG:
                loadw(j + 1)
            s = j % 2
            for l in range(NL + 1):
                c0, c1 = lchunk(l)
                N = c1 - c0
                hs = hd_i % 2
                hd_i += 1
                for hc in range(4):
                    bank = nextbank(0, 4)
                    ps = cx.ps[bank]
                    P.group("tensor", [(lambda e, k=k, ps=ps, s=s, hc=hc, c0=c0, c1=c1, N=N: e.matmul(ps[:, 0:N], WU[s][:, k, hc * 128:(hc + 1) * 128], HN[:, k, c0:c1],
                                                                                                  start=(k == 0), stop=(k == 7))) for k in range(8)],
                            reads=[b_WU[s], b_HN[l]], writes=[cx.psb[bank]])
                    rs = rl_i % 2
                    rl_i += 1
                    P.op("scalar", (lambda e, ps=ps, rs=rs, N=N: e.activation(out=RL[rs][:, 0:N], in_=ps[:, 0:N], func=AF.Relu)),
                         reads=[cx.psb[bank]], writes=[b_RL[rs]])
                    P.op("vector", (lambda e, rs=rs, hs=hs, hc=hc, N=N: e.tensor_tensor(out=HD[hs][:, hc, 0:N], in0=RL[rs][:, 0:N], in1=RL[rs][:, 0:N], op=ALU.mult)),
                         reads=[b_RL[rs]], writes=[b_HD[hs]])
                for m in range(8):
                    bank = nextbank(4, 8)
                    ps = cx.ps[bank]
                    P.group("tensor", [(lambda e, hc=hc, ps=ps, s=s, hs=hs, m=m, N=N: e.matmul(ps[:, 0:N], WD[s][:, hc, m * 128:(m + 1) * 128], HD[hs][:, hc, 0:N],
                                                                                            start=(hc == 0), stop=(hc == 3))) for hc in range(4)],
                            reads=[b_WD[s], b_HD[hs]], writes=[cx.psb[bank]])
                    P.op("vector", (lambda e, ps=ps, m=m, c0=c0, c1=c1, N=N: e.tensor_tensor(out=H[:, m, c0:c1], in0=H[:, m, c0:c1], in1=ps[:, 0:N], op=ALU.add)),
                         reads=[cx.psb[bank]], writes=[b_H[l]])
        P.barrier()
        stm.close()

    if mode != "B":
        open_H(xTh_d[:, :, :], "a")
        load_WA(w_o_d[:, :, :])
        proj_add(OT, b_OT)
        mlp(0, V_GMLP0)
        H, b_H = hctx["H"], hctx["b_H"]
        sp_tok = P.op("sync", lambda e, H=H: e.dma_start(out=hsp_h.ap(), in_=H[:].rearrange("p k t -> p (k t)")), reads=b_H, dma_sem=P.newsem("d_hsp"))
        norm_all(V_GMIX1)
        P.barrier(extra=[sp_tok])
        hctx["st"].close()

    st3 = ExitStack()
    A3 = lambda name, shape, dt: st3.enter_context(nc.sbuf_tensor(name, shape, dt))
    X = [A3(f"X{i}", [128, TH], F32) for i in range(4)]
    b_X = [Buf(f"X{i}") for i in range(4)]
    PG = A3("PG", [128, 8, TH], BF16)
    b_PG = [Buf(f"PG{c}") for c in range(8)]
    b_Y = [Buf(f"Y{c}") for c in range(8)]
    UB = A3("UB", [128, TH], BF16)
    b_UB = Buf("UB")
    WI = [A3(f"WI{s}", [128, 8, 256], BF16) for s in range(2)]
    b_WI = [Buf(f"WI{s}") for s in range(2)]
    s_WI = [P.newsem(f"d_WI{s}") for s in range(2)]
    WG = A3("WG", [128, 8, 256], BF16)
    b_WG = Buf("WG", track_reads=False)
    sm = A3("sm", [128, 64], F32)
    b_sm = Buf("sm")
    idx_sb = A3("idx_sb", [128, 1], I32)
    b_idx = Buf("idx", track_reads=False)
    CC = lambda c: sm[:, c:c + 1]
    NBR = lambda c: sm[:, 8 + c:9 + c]
    NBI = lambda c: sm[:, 16 + c:17 + c]
    ST = sm[:, 24:40]
    SG = sm[:, 40:56]
    SE = sm[:, 56:64]
    if mode != "B":
        s_wg = P.newsem("d_wg")
        P.op("gpsimd", lambda e: e.dma_start(out=WG[:, :, 0:128], in_=w_rg_d[:, :, :]), dma_sem=s_wg)
        tk = P.op("gpsimd", lambda e: e.dma_start(out=WG[:, :, 128:256], in_=w_ig_d[:, :, :]), dma_sem=s_wg)
        b_WG.wr = tk
    if mode == "full":
        P.op("sync", lambda e: e.dma_start(out=idx_sb[:], in_=idx_d[:, :]), writes=[b_idx], dma_sem=P.newsem("d_idx"))
    lamv = vecs[:, V_LAM * 8:V_LAM * 8 + 8]
    P.op("scalar", lambda e: e.activation(out=sm[:, 0:8], in_=lamv, func=AF.Exp, scale=-1.0), reads=[b_vecs], writes=[b_sm])
    P.op("scalar", lambda e: e.activation(out=sm[:, 0:8], in_=sm[:, 0:8], func=AF.Ln, bias=cx.cvec[:, 0:1], scale=1.0), reads=[cx.b_cvec], writes=[b_sm])
    P.op("vector", lambda e: e.tensor_scalar(out=sm[:, 0:8], in0=sm[:, 0:8], scalar1=-8.0, scalar2=None, op0=ALU.mult), writes=[b_sm])
    P.op("vector", lambda e: e.tensor_scalar(out=sm[:, 8:16], in0=vecs[:, V_BRG * 8:V_BRG * 8 + 8], scalar1=-1.0, scalar2=None, op0=ALU.mult), reads=[b_vecs], writes=[b_sm])
    P.op("vector", lambda e: e.tensor_scalar(out=sm[:, 16:24], in0=vecs[:, V_BIG * 8:V_BIG * 8 + 8], scalar1=-1.0, scalar2=None, op0=ALU.mult), reads=[b_vecs], writes=[b_sm])
    P.op("gpsimd", lambda e: e.memset(PG[:, :, 0:16], 0.0), writes=b_PG)

    def load_wi(c):
        s = c % 2
        P.op("gpsimd", (lambda e, s=s, c=c: e.dma_start(out=WI[s][:, :, 0:128], in_=w_in_d[:, :, c * 128:(c + 1) * 128])), writes=[b_WI[s]], dma_sem=s_WI[s])
        P.op("gpsimd", (lambda e, s=s, c=c: e.dma_start(out=WI[s][:, :, 128:256], in_=w_in_d[:, :, 1024 + c * 128:1024 + (c + 1) * 128])), writes=[b_WI[s]], dma_sem=s_WI[s])

    if mode != "B":
        X1, X2, X3, X4 = X
        bX1, bX2, bX3, bX4 = b_X
        load_wi(0)
        for c in range(8):
            if c + 1 < 8:
                load_wi(c + 1)
            s = c % 2
            for l in range(NL + 1):
                c0, c1 = lchunk(l)
                N = c1 - c0
                bank = nextbank()
                ps = cx.ps[bank]
                P.group("tensor", [(lambda e, k=k, ps=ps, s=s, c0=c0, c1=c1, N=N: e.matmul(ps[:, 0:N], WI[s][:, k, 128:256], HN[:, k, c0:c1], start=(k == 0), stop=(k == 7)))
                                   for k in range(8)], reads=[b_WI[s], b_HN[l]], writes=[cx.psb[bank]])
                P.op("scalar", (lambda e, ps=ps, c0=c0, c1=c1, N=N: e.activation(out=X1[:, c0:c1], in_=ps[:, 0:N], func=AF.Copy)), reads=[cx.psb[bank]], writes=[bX1])
            P.op("vector", (lambda e, c=c: e.tensor_scalar(out=X2[:, :], in0=X1[:, :], scalar1=vcol(V_CW3, c), scalar2=vcol(V_CB, c), op0=ALU.mult, op1=ALU.add)),
                 reads=[bX1, b_vecs], writes=[bX2])
            for j, vv in ((1, V_CW2), (2, V_CW1), (3, V_CW0)):
                P.op("vector", (lambda e, c=c, j=j, vv=vv: e.scalar_tensor_tensor(out=X2[:, j:TH], in0=X1[:, 0:TH - j], scalar=vcol(vv, c), in1=X2[:, j:TH],
                                                                               op0=ALU.mult, op1=ALU.add)), reads=[bX1], writes=[bX2])
            P.op("gpsimd", lambda e: e.tensor_copy(out=UB[:, :], in_=X2[:, :]), reads=[bX2], writes=[b_UB])
            for (Xg, bXg, off, nb) in ((X3, bX3, 0, NBR), (X4, bX4, 128, NBI)):
                for l in range(NL + 1):
                    c0, c1 = lchunk(l)
                    N = c1 - c0
                    bank = nextbank()
                    ps = cx.ps[bank]
                    P.op("tensor", (lambda e, ps=ps, c=c, off=off, c0=c0, c1=c1, N=N: e.matmul(ps[:, 0:N], WG[:, c, off:off + 128], UB[:, c0:c1], start=True, stop=True)),
                         reads=[b_WG, b_UB], writes=[cx.psb[bank]])
                    P.op("scalar", (lambda e, ps=ps, Xg=Xg, nb=nb, c=c, c0=c0, c1=c1, N=N: e.activation(out=Xg[:, c0:c1], in_=ps[:, 0:N], func=AF.Exp, bias=nb(c), scale=-1.0)),
                         reads=[cx.psb[bank], b_sm], writes=[bXg])
                P.op("vector", (lambda e, Xg=Xg: e.tensor_scalar(out=Xg[:, :], in0=Xg[:, :], scalar1=1.0, scalar2=None, op0=ALU.add)), writes=[bXg])
                P.op("vector", (lambda e, Xg=Xg: e.reciprocal(out=Xg[:, :], in_=Xg[:, :])), writes=[bXg])
            P.op("scalar", (lambda e, c=c: e.activation(out=X3[:, :], in_=X3[:, :], func=AF.Exp, scale=CC(c))), reads=[b_sm], writes=[bX3])
            P.op("gpsimd", lambda e: e.tensor_tensor(out=X1[:, :], in0=X3[:, :], in1=X3[:, :], op=ALU.mult), reads=[bX3], writes=[bX1])
            P.op("scalar", lambda e: e.activation(out=X1[:, :], in_=X1[:, :], func=AF.Ln, bias=cx.cvec[:, 0:1], scale=-1.0), reads=[cx.b_cvec], writes=[bX1])
            P.op("scalar", lambda e: e.activation(out=X1[:, :], in_=X1[:, :], func=AF.Exp, scale=0.5), writes=[bX1])
            P.op("gpsimd", lambda e: e.tensor_tensor(out=X4[:, :], in0=X4[:, :], in1=X2[:, :], op=ALU.mult), reads=[bX2], writes=[bX4])
            P.op("vector", lambda e: e.tensor_tensor(out=X4[:, :], in0=X4[:, :], in1=X1[:, :], op=ALU.mult), reads=[bX1], writes=[bX4])
            P.op("vector", lambda e: e.tensor_tensor_scan(out=X1[:, :], data0=X3[:, :], data1=X4[:, :], initial=0.0, op0=ALU.mult, op1=ALU.add),
                 reads=[bX3, bX4], writes=[bX1])
            P.op("vector", lambda e: e.tensor_tensor_scan(out=X2[:, 16:TH], data0=X3[:, 16:TH], data1=X3[:, 16:TH], initial=1.0, op0=ALU.mult, op1=ALU.min),
                 reads=[bX3], writes=[bX2])
            P.op("vector", (lambda e, c=c: e.tensor_copy(out=sm[:, 24 + c:25 + c], in_=X1[:, TH - 1:TH])), reads=[bX1], writes=[b_sm])
            P.op("vector", (lambda e, c=c: e.tensor_copy(out=sm[:, 32 + c:33 + c], in_=X1[:, 15:16])), reads=[bX1], writes=[b_sm])
            for l in range(NL + 1):
                c0, c1 = lchunk(l)
                N = c1 - c0
                bank = nextbank()
                ps = cx.ps[bank]
                P.group("tensor", [(lambda e, k=k, ps=ps, s=s, c0=c0, c1=c1, N=N: e.matmul(ps[:, 0:N], WI[s][:, k, 0:128], HN[:, k, c0:c1], start=(k == 0), stop=(k == 7)))
                                   for k in range(8)], reads=[b_WI[s], b_HN[l]], writes=[cx.psb[bank]])
                P.op("scalar", (lambda e, ps=ps, c0=c0, c1=c1, N=N: e.activation(out=X3[:, c0:c1], in_=ps[:, 0:N], func=AF.Copy)), reads=[cx.psb[bank]], writes=[bX3])
            P.op("gpsimd", lambda e: e.tensor_tensor(out=X4[:, :], in0=X3[:, :], in1=X3[:, :], op=ALU.mult), reads=[bX3], writes=[bX4])
            P.op("vector", lambda e: e.tensor_scalar(out=X4[:, :], in0=X4[:, :], scalar1=0.044715, scalar2=1.0, op0=ALU.mult, op1=ALU.add), writes=[bX4])
            P.op("gpsimd", lambda e: e.tensor_tensor(out=X4[:, :], in0=X4[:, :], in1=X3[:, :], op=ALU.mult), reads=[bX3], writes=[bX4])
            P.op("scalar", lambda e: e.activation(out=X4[:, :], in_=X4[:, :], func=AF.Exp, scale=-1.5957691216057308), writes=[bX4])
            P.op("vector", lambda e: e.tensor_scalar(out=X4[:, :], in0=X4[:, :], scalar1=1.0, scalar2=None, op0=ALU.add), writes=[bX4])
            P.op("vector", lambda e: e.reciprocal(out=X4[:, :], in_=X4[:, :]), writes=[bX4])
            P.op("gpsimd", lambda e: e.tensor_tensor(out=X4[:, :], in0=X4[:, :], in1=X3[:, :], op=ALU.mult), reads=[bX3], writes=[bX4])
            P.op("vector", (lambda e, c=c: e.tensor_tensor(out=OT[:, c, :], in0=X1[:, :], in1=X4[:, :], op=ALU.mult)), reads=[bX1, bX4], writes=[b_Y[c]] + b_OT)
            P.op("gpsimd", (lambda e, c=c: e.tensor_tensor(out=PG[:, c, 16:TH], in0=X2[:, 16:TH], in1=X4[:, 16:TH], op=ALU.mult)), reads=[bX2, bX4], writes=[b_PG[c]])
    b_sg = Buf("sg")
    if mode == "full":
        s_st = P.newsem("d_st")
        tk = P.op("gpsimd", lambda e: e.dma_start(out=st_in_h.ap(), in_=ST), reads=[b_sm], dma_sem=s_st)
        s_cc = P.newsem("cc_st")
        s_cc.n += 1
        cc_tok = (s_cc, s_cc.n, None)
        P.q["gpsimd"].append((lambda e: e.collective_compute("AllGather", ALU.bypass, replica_groups=[list(range(n_ranks))],
                                                             ins=[st_in_h.ap().opt()], outs=[st_all_h.ap().opt()]), [tk], (s_cc, 1), False))
        P.op("gpsimd", lambda e: e.indirect_dma_start(out=SG, out_offset=None, in_=st_all_h.ap(),
                                                      in_offset=bass.IndirectOffsetOnAxis(ap=idx_sb[:, 0:1], axis=0)),
             reads=[b_idx], writes=[b_sg], dma_sem=P.newsem("d_sg"), extra_waits=[cc_tok])
    elif mode == "A":
        s_oa = P.newsem("d_outA")
        t1 = P.op("sync", lambda e: e.dma_start(out=yl_h.ap(), in_=OT[:].rearrange("p k t -> p (k t)")), reads=b_Y + b_OT, dma_sem=s_oa)
        t2 = P.op("sync", lambda e: e.dma_start(out=pg_h.ap(), in_=PG[:].rearrange("p k t -> p (k t)")), reads=b_PG, dma_sem=s_oa)
        t3 = P.op("sync", lambda e: e.dma_start(out=sto_h.ap(), in_=ST), reads=[b_sm], dma_sem=s_oa)
        P.op("sync", lambda e: e.wait_ge(s_oa.h, s_oa.n), extra_waits=[t3], sig=False)
        P.barrier(extra=[t3])
        st3.close()
        st.close()
        return
    else:
        s_ib = P.newsem("d_inB")
        P.op("sync", lambda e: e.dma_start(out=OT[:].rearrange("p k t -> p (k t)"), in_=yl_h.ap()), writes=b_Y + b_OT, dma_sem=s_ib)
        P.op("sync", lambda e: e.dma_start(out=PG[:].rearrange("p k t -> p (k t)"), in_=pg_h.ap()), writes=b_PG, dma_sem=s_ib)
        P.op("sync", lambda e: e.dma_start(out=ST, in_=sto_h.ap()), writes=[b_sm], dma_sem=s_ib)
        tk = P.op("sync", lambda e: e.dma_start(out=SG, in_=sgi_h.ap()), writes=[b_sg], dma_sem=s_ib)
        for bb in b_Y + b_OT + b_PG + [b_sm, b_sg]:
            bb.wr = tk
    P.op("vector", lambda e: e.tensor_tensor(out=SE, in0=sm[:, 40:48], in1=sm[:, 32:40], op=ALU.subtract), reads=[b_sg, b_sm], writes=[b_sm])
    P.op("vector", lambda e: e.tensor_scalar(out=SE, in0=SE, scalar1=FLAGB, scalar2=None, op0=ALU.mult), reads=[b_vecs], writes=[b_sm])
    for c in range(8):
        P.op("vector", (lambda e, c=c: e.scalar_tensor_tensor(out=OT[:, c, 16:TH], in0=PG[:, c, 16:TH], scalar=sm[:, 56 + c:57 + c], in1=OT[:, c, 16:TH],
                                                             op0=ALU.mult, op1=ALU.add)), reads=[b_PG[c], b_sm], writes=[b_Y[c]] + b_OT)
    P.barrier()
    st3.close()
    open_H(hsp_h.ap().rearrange("p (k t) -> p k t", k=8), "b")
    load_WA(w_out_d[:, :, :])
    proj_add(OT, b_OT)
    cx.dbg_after_lru = (hctx["H"], hctx["b_H"])

    mlp(1, V_GMLP1)
    H, b_H = hctx["H"], hctx["b_H"]
    stf = ExitStack()
    OS = [stf.enter_context(nc.sbuf_tensor(f"OS{s}", [128, 8, 512], F32)) for s in range(2)]
    b_OS = [Buf(f"OS{s}") for s in range(2)]
    s_OS = [P.newsem(f"d_OS{s}") for s in range(2)]
    otoks = {}
    for l in range(NL + 1):
        c0, c1 = lchunk(l)
        N = c1 - c0
        o = l % 2
        emit_norm(cx, (lambda k, c0=c0, c1=c1: H[:, k, c0:c1]), b_H[l], N, (lambda k: vcol(V_GFIN, k)), b_vecs,
                  (lambda k, o=o, N=N: OS[o][:, k, 0:N]), b_OS[o], (lambda k, N=N: sq[:, k, 0:N]), b_sq,
                  tmpa[:, 0:N], b_tmpa, rstd[:, 0:N], b_rstd, psbank=nextbank())
        otoks[o] = P.op("sync", (lambda e, o=o, c0=c0, c1=c1, N=N: e.dma_start(out=outT_d[:, :, c0:c1], in_=OS[o][:, :, 0:N])), reads=[b_OS[o]], dma_sem=s_OS[o])
    P.op("sync", lambda e: e.wait_ge(s_OS[0].h, s_OS[0].n), extra_waits=list(otoks.values()), sig=False)
    P.barrier(extra=list(otoks.values()))
    stf.close()
    hctx["st"].close()
    st.close()


def build_phase2_program(n_ranks=8, mode="full"):
    nc = bass.Bass("TRN2", target_bir_lowering=False)
    with ExitStack() as es:
        P = Prog(nc, es)
        cx = setup_common(nc, es, P)
        oTh_d = None
        if mode != "B":
            oTh_d = nc.dram_tensor("oTh", [D, TH], BF16, kind="ExternalInput").ap().rearrange("(k p) t -> p k t", p=128)

        def load_o(OT, b_OT):
            tk = P.op("sync", lambda e: e.dma_start(out=OT[:], in_=oTh_d[:, :, :]), dma_sem=P.newsem("d_oT"))
            for b in b_OT:
                b.wr = tk

        emit_phase234(cx, load_o, n_ranks=n_ranks, mode=mode)
        block = es.enter_context(nc.Block())
        P.replay(block)
    return nc


def pack_vecs(inp, flagb):
    cols = []
    f = lambda v: np.ascontiguousarray(np.asarray(v, np.float32).reshape(8, 128).T)
    cols.append(f(inp["norm_mix"][0])); cols.append(f(inp["norm_mlp"][0])); cols.append(f(inp["norm_mix"][1]))
    cols.append(f(inp["norm_mlp"][1])); cols.append(f(inp["norm_final"]))
    for j in range(4):
        cols.append(f(inp["lru_conv_w"][0][j]))
    cols.append(f(inp["lru_conv_b"][0])); cols.append(f(inp["lru_b_rg"][0])); cols.append(f(inp["lru_b_ig"][0])); cols.append(f(inp["lru_lambda"][0]))
    fl = np.full((128, 8), float(flagb), np.float32)
    return np.ascontiguousarray(np.concatenate(cols + [fl], axis=1))


def phase2_inputs(inp, core, xT_full, oT_full, tok0, n_ranks=8):
    role = core % 2
    prev = (core - 1) % n_ranks
    return {
        "xTh": np.ascontiguousarray(xT_full[:, tok0:tok0 + TH]),
        "oTh": np.ascontiguousarray(oT_full[:, tok0:tok0 + TH]),
        "w_o": inp["sb_w_o"][0], "w_up0": inp["mlp_w_up"][0], "w_dn0": inp["mlp_w_down"][0],
        "w_up1": inp["mlp_w_up"][1], "w_dn1": inp["mlp_w_down"][1],
        "w_in": inp["lru_w_in"][0], "w_out": inp["lru_w_out"][0], "w_rg": inp["lru_w_rg"][0], "w_ig": inp["lru_w_ig"][0],
        "vecs": pack_vecs(inp, role),
        "idx_st": (prev * 128 + np.arange(128, dtype=np.int32)).reshape(128, 1),
        "consts": make_consts(),
    }


_PROGS = {}


def kernel(**inputs):
    inp = {k: np.asarray(v) for k, v in inputs.items()}
    ncores = 8
    set_nch(9)
    set_nl(4)
    if "p1" not in _PROGS:
        _PROGS["p1"] = build_phase1_program()
        _PROGS["pA"] = build_phase2_program(mode="A")
        _PROGS["pB"] = build_phase2_program(mode="B")
    consts = make_consts()
    xT = [np.ascontiguousarray(np.concatenate([inp["meta_tokens"], inp["x"][b]], 0).T.astype(np.float32)) for b in range(4)]
    w = inp["sb_w_qkv"][0]
    g0 = np.ascontiguousarray(inp["norm_mix"][0].reshape(8, 128).T.astype(np.float32))
    maps1 = []
    for core in range(ncores):
        b, g = core // 2, core % 2
        wq = np.concatenate([w[:, 512 * g:512 * g + 512], w[:, 1024 + 512 * g:1024 + 512 * g + 512],
                             w[:, 2048 + 512 * g:2048 + 512 * g + 512]], 1)
        maps1.append({"xT": xT[b], "wqkv": np.ascontiguousarray(wq), "g_mix0": g0, "consts": consts})
    r1 = run_bass_kernel_spmd(_PROGS["p1"], maps1, core_ids=list(range(ncores)))
    oT = [np.concatenate([np.asarray(r1.results[2 * b]["oT"]), np.asarray(r1.results[2 * b + 1]["oT"])], 0) for b in range(4)]
    KA = ["xTh", "oTh", "w_o", "w_up0", "w_dn0", "w_in", "w_rg", "w_ig", "vecs", "consts"]
    KB = ["w_out", "w_up1", "w_dn1", "vecs", "consts"]
    full = [phase2_inputs(inp, core, xT[core // 2], oT[core // 2], 2048 * (core % 2), n_ranks=ncores) for core in range(ncores)]
    rA = run_bass_kernel_spmd(_PROGS["pA"], [{k: d[k] for k in KA} for d in full], core_ids=list(range(ncores)))
    mapsB = []
    for core in range(ncores):
        m = {k: full[core][k] for k in KB}
        for k in ("yl", "pg", "hsp", "st_own"):
            m[k] = np.asarray(rA.results[core][k])
        m["sg"] = np.asarray(rA.results[(core - 1) % ncores]["st_own"])
        mapsB.append(m)
    rB = run_bass_kernel_spmd(_PROGS["pB"], mapsB, core_ids=list(range(ncores)))
    out = np.empty((4, 4096, D), np.float32)
    for core in range(ncores):
        b, role = core // 2, core % 2
        o = np.asarray(rB.results[core]["outT"])
        out[b, 2048 * role:2048 * role + 2048, :] = o[:, 16:].T
    return out
```
